# Optimizing a Trainium2 kernel written in Bass

```python
import math
import jax, jax.numpy as jnp
from jax import lax
import numpy as np

D_MODEL = 1024
BATCH = 32
SEQ = 2048
DEPTH = 1

SSM_GROUP = 16
SSM_WIDTH = D_MODEL // 2
SSM_GROUPS = SSM_WIDTH // SSM_GROUP
SSM_STATE = 64
CONV_WIDTH = D_MODEL
CONV_K = 3
D_FF = 4 * D_MODEL
NORM_EPS = 1e-6
DT_MIN = 1e-3
DT_MAX = 1e-1
SPLIT_SIZES = (SSM_WIDTH, CONV_WIDTH, CONV_WIDTH, CONV_WIDTH, D_MODEL, D_MODEL)
IN_COLS = sum(SPLIT_SIZES)
SPLIT_POINTS = tuple(int(v) for v in np.cumsum(SPLIT_SIZES)[:-1])

kernel_name = "hybrid_s5_shortconv_gated_block"


def rmsnorm(x, g):
    xf = x.astype(jnp.float32)
    var = jnp.mean(xf * xf, axis=-1, keepdims=True)
    return (xf * lax.rsqrt(var + NORM_EPS) * g.astype(jnp.float32)).astype(x.dtype)


def _ssm_combine(e1, e2):
    a1r, a1i, b1r, b1i = e1
    a2r, a2i, b2r, b2i = e2
    ar = a1r * a2r - a1i * a2i
    ai = a1r * a2i + a1i * a2r
    br = a2r * b1r - a2i * b1i + b2r
    bi = a2r * b1i + a2i * b1r + b2i
    return (ar, ai, br, bi)


def s5_branch(u, lam_re, lam_im, log_dt, b_re, b_im, c_re, c_im, d_skip):
    f32 = jnp.float32
    bsz, seq, _ = u.shape
    uf = u.astype(f32).reshape(bsz, seq, SSM_GROUPS, SSM_GROUP)
    lr = lam_re.astype(f32)
    li = lam_im.astype(f32)
    dt = jnp.exp(log_dt.astype(f32))[:, None]
    mag = jnp.exp(lr * dt)
    ab_re = mag * jnp.cos(li * dt)
    ab_im = mag * jnp.sin(li * dt)
    er = ab_re - 1.0
    ei = ab_im
    den = lr * lr + li * li
    q_re = (er * lr + ei * li) / den
    q_im = (ei * lr - er * li) / den
    br = b_re.astype(f32)
    bi = b_im.astype(f32)
    bb_re = q_re[..., None] * br - q_im[..., None] * bi
    bb_im = q_re[..., None] * bi + q_im[..., None] * br
    bu_re = jnp.einsum('gnc,bsgc->bsgn', bb_re, uf)
    bu_im = jnp.einsum('gnc,bsgc->bsgn', bb_im, uf)
    a_re = jnp.broadcast_to(ab_re[None, None], (1, seq, SSM_GROUPS, SSM_STATE))
    a_im = jnp.broadcast_to(ab_im[None, None], (1, seq, SSM_GROUPS, SSM_STATE))
    _, _, s_re, s_im = lax.associative_scan(_ssm_combine, (a_re, a_im, bu_re, bu_im), axis=1)
    y = (jnp.einsum('gcn,bsgn->bsgc', c_re.astype(f32), s_re)
         - jnp.einsum('gcn,bsgn->bsgc', c_im.astype(f32), s_im))
    y = y + d_skip.astype(f32).reshape(SSM_GROUPS, SSM_GROUP) * uf
    return y.reshape(bsz, seq, SSM_WIDTH)


def causal_short_conv(v, w, b):
    seq = v.shape[1]
    vp = jnp.pad(v, ((0, 0), (CONV_K - 1, 0), (0, 0)))
    out = b
    for k in range(CONV_K):
        out = out + w[k] * vp[:, k:k + seq]
    return out


def mixer_block(xn, w_in, b_in, lam_re, lam_im, log_dt, b_re, b_im, c_re, c_im, d_skip,
                w_glu_a, w_glu_b, conv_w, conv_b, w_conv_out, w_out):
    proj = jnp.einsum('bsd,de->bse', xn, w_in) + b_in
    u_ssm, c_bgate, c_cgate, c_val, g_ssm, g_conv = jnp.split(proj, SPLIT_POINTS, axis=-1)
    y_ssm = s5_branch(u_ssm, lam_re, lam_im, log_dt, b_re, b_im, c_re, c_im, d_skip).astype(xn.dtype)
    z = jax.nn.gelu(y_ssm)
    y_a = jnp.einsum('bse,ed->bsd', z, w_glu_a) * jax.nn.sigmoid(jnp.einsum('bse,ed->bsd', z, w_glu_b))
    y_b = jnp.einsum('bse,ed->bsd', c_bgate * causal_short_conv(c_cgate * c_val, conv_w, conv_b), w_conv_out)
    merged = jax.nn.sigmoid(g_ssm) * y_a + jax.nn.sigmoid(g_conv) * y_b
    return jnp.einsum('bsd,de->bse', merged, w_out)


def squared_relu_mlp(xn, w_ff1, w_ff2):
    h = jax.nn.relu(jnp.einsum('bsd,df->bsf', xn, w_ff1))
    return jnp.einsum('bsf,fd->bsd', h * h, w_ff2)


def setup_inputs(seed: int = 0) -> dict:
    key = jax.random.key(seed)
    ks = jax.random.split(key, 24)
    L, D, G, N, C = DEPTH, D_MODEL, SSM_GROUPS, SSM_STATE, SSM_GROUP
    nrm = jax.random.normal
    x = nrm(ks[0], (BATCH, SEQ, D), jnp.float32)
    norm_mix_g = 1.0 + 0.02 * nrm(ks[1], (L, D), jnp.float32)
    w_in = nrm(ks[2], (L, D, IN_COLS), jnp.float32) * D ** -0.5
    b_in = 0.02 * nrm(ks[3], (L, IN_COLS), jnp.float32)
    n_idx = jnp.arange(N, dtype=jnp.float32)
    lam_re = -0.5 + 0.01 * nrm(ks[4], (L, G, N), jnp.float32)
    lam_im = math.pi * n_idx[None, None, :] + 0.01 * nrm(ks[5], (L, G, N), jnp.float32)
    log_dt = jax.random.uniform(ks[6], (L, G), jnp.float32, math.log(DT_MIN), math.log(DT_MAX))
    ssm_b_re = nrm(ks[7], (L, G, N, C), jnp.float32) * (2.0 * C) ** -0.5
    ssm_b_im = nrm(ks[8], (L, G, N, C), jnp.float32) * (2.0 * C) ** -0.5
    ssm_c_re = nrm(ks[9], (L, G, C, N), jnp.float32) * (2.0 * N) ** -0.5
    ssm_c_im = nrm(ks[10], (L, G, C, N), jnp.float32) * (2.0 * N) ** -0.5
    ssm_d = 1.0 + 0.1 * nrm(ks[11], (L, SSM_WIDTH), jnp.float32)
    w_glu_a = nrm(ks[12], (L, SSM_WIDTH, D), jnp.float32) * SSM_WIDTH ** -0.5
    w_glu_b = nrm(ks[13], (L, SSM_WIDTH, D), jnp.float32) * SSM_WIDTH ** -0.5
    conv_w = nrm(ks[14], (L, CONV_K, CONV_WIDTH), jnp.float32) * CONV_K ** -0.5
    conv_b = 0.02 * nrm(ks[15], (L, CONV_WIDTH), jnp.float32)
    w_conv_out = nrm(ks[16], (L, CONV_WIDTH, D), jnp.float32) * CONV_WIDTH ** -0.5
    w_out = nrm(ks[17], (L, D, D), jnp.float32) * D ** -0.5
    norm_mlp_g = 1.0 + 0.02 * nrm(ks[18], (L, D), jnp.float32)
    w_ff1 = nrm(ks[19], (L, D, D_FF), jnp.float32) * D ** -0.5
    w_ff2 = nrm(ks[20], (L, D_FF, D), jnp.float32) * D_FF ** -0.5
    norm_final_g = 1.0 + 0.02 * nrm(ks[21], (D,), jnp.float32)
    return {"x": x, "norm_mix_g": norm_mix_g, "w_in": w_in, "b_in": b_in,
            "lam_re": lam_re, "lam_im": lam_im, "log_dt": log_dt,
            "ssm_b_re": ssm_b_re, "ssm_b_im": ssm_b_im, "ssm_c_re": ssm_c_re, "ssm_c_im": ssm_c_im,
            "ssm_d": ssm_d, "w_glu_a": w_glu_a, "w_glu_b": w_glu_b,
            "conv_w": conv_w, "conv_b": conv_b, "w_conv_out": w_conv_out, "w_out": w_out,
            "norm_mlp_g": norm_mlp_g, "w_ff1": w_ff1, "w_ff2": w_ff2, "norm_final_g": norm_final_g}


def reference(x, norm_mix_g, w_in, b_in, lam_re, lam_im, log_dt, ssm_b_re, ssm_b_im, ssm_c_re, ssm_c_im,
              ssm_d, w_glu_a, w_glu_b, conv_w, conv_b, w_conv_out, w_out, norm_mlp_g, w_ff1, w_ff2,
              norm_final_g):
    h = x
    for l in range(DEPTH):
        xn = rmsnorm(h, norm_mix_g[l])
        h = h + mixer_block(xn, w_in[l], b_in[l], lam_re[l], lam_im[l], log_dt[l],
                            ssm_b_re[l], ssm_b_im[l], ssm_c_re[l], ssm_c_im[l], ssm_d[l],
                            w_glu_a[l], w_glu_b[l], conv_w[l], conv_b[l], w_conv_out[l], w_out[l])
        xn = rmsnorm(h, norm_mlp_g[l])
        h = h + squared_relu_mlp(xn, w_ff1[l], w_ff2[l])
    return rmsnorm(h, norm_final_g)
```

```python
import numpy as np
import ml_dtypes
from contextlib import ExitStack

import concourse.bass as bass
import concourse.mybir as mybir
from concourse.bass_utils import run_bass_kernel_spmd

F32 = mybir.dt.float32
BF16 = mybir.dt.bfloat16
AF = mybir.ActivationFunctionType
ALU = mybir.AluOpType

NCORES = 8
NTOK = 8192
D = 1024
NPASS = 2
PTOK = 4096
TT = 512
NE = 71
TWO_PI = 2.0 * np.pi
OFF_S = np.pi + TWO_PI * 128.0
OFF_C = OFF_S + 0.5 * np.pi
SIN_SC = 1.0 - 2e-6
OPSW = 3840
NUNITS = 35
RING = 4

DEBUG = {}
STOP_AFTER = None


class Tile:
    __slots__ = ("name", "w", "r")

    def __init__(self, name):
        self.name = name
        self.w = None
        self.r = []


class Op:
    __slots__ = ("eng", "fn", "deps", "dma", "key", "need", "sig", "ph")


class Prog:
    ENG = ("pe", "act", "dve", "pool", "sp")

    def __init__(self, nc):
        self.nc = nc
        self.esem = {e: nc.alloc_semaphore("sem_" + e) for e in self.ENG}
        self.ecnt = {e: 0 for e in self.ENG}
        self.dsem = {}
        self.waited = {e: {} for e in self.ENG}
        self.ops = {e: [] for e in self.ENG}
        self.nops = 0
        self.phase = 0
        self.pool = [nc.alloc_semaphore("dsem_%d" % i) for i in range(72)]
        allsems = list(self.esem.values()) + self.pool
        with nc.Block() as block:
            def clr(eng):
                for s_ in allsems:
                    eng.sem_clear(s_)
            block.sync(clr)

    def op(self, eng, fn, reads=(), writes=(), dma=False, key=None):
        import os
        lim = os.environ.get("KERNEL_OPLIMIT")
        if lim is not None and self.nops >= int(lim):
            return None
        o = Op()
        o.eng, o.fn, o.dma, o.key, o.need, o.sig = eng, fn, dma, key, False, None
        o.ph = self.phase
        deps = []

        def add(d, raw):
            if d is None or d.ph != self.phase:
                return
            if d.eng == eng and (not d.dma) and (not dma) and (not raw) and eng != "pool":
                return
            if d.dma and dma and d.key == key and isinstance(key, str) and key.startswith("G:"):
                return
            if d not in deps:
                deps.append(d)

        for t in reads:
            add(t.w, True)
        for t in writes:
            add(t.w, False)
            for r in t.r:
                add(r, False)
        for d in deps:
            d.need = True
        o.deps = deps
        for t in reads:
            if not dma:
                t.r = [r for r in t.r if not (r.eng == eng and not r.dma)]
            t.r.append(o)
        for t in writes:
            t.w = o
            t.r = []
        self.ops[eng].append(o)
        self.nops += 1
        return o

    def flush(self):
        nc = self.nc
        groups = {}
        for e in self.ENG:
            for o in self.ops[e]:
                if o.dma:
                    ent = self.dsem.get(o.key)
                    if ent is None:
                        ent = [self.pool[len(self.dsem)], 0]
                        self.dsem[o.key] = ent
                    ent[1] += 16
                    o.sig = (ent[0], ent[1])
                    if isinstance(o.key, str) and o.key.startswith("G:"):
                        groups.setdefault(o.key, []).append(o)
                elif o.need:
                    self.ecnt[e] += 1
                    o.sig = (self.esem[e], self.ecnt[e])
        for k, lst in groups.items():
            fin = self.dsem[k][1]
            for o in lst:
                o.sig = (o.sig[0], fin)

        def run(e, eng):
            w = self.waited[e]
            used = {}
            for o in self.ops[e]:
                for d in o.deps:
                    s, v = d.sig
                    if w.get(s.num, 0) < v:
                        eng.wait_ge(s, v)
                        w[s.num] = v
                ins = o.fn(eng)
                if o.sig is not None:
                    ins.then_inc(o.sig[0], 16 if o.dma else 1)
                    if o.dma:
                        used[o.sig[0].num] = (o.sig[0], max(o.sig[1], used.get(o.sig[0].num, (None, 0))[1]))
            for num, (s, v) in used.items():
                if w.get(num, 0) < v:
                    eng.wait_ge(s, v)
                    w[num] = v

        with nc.Block() as block:
            reg = {"pe": block.tensor, "act": block.scalar, "dve": block.vector,
                   "pool": block.gpsimd, "sp": block.sync}
            for e in self.ENG:
                if self.ops[e]:
                    reg[e](lambda eng, e=e: run(e, eng))
        self.ops = {e: [] for e in self.ENG}
        self.phase += 1


def interleave(gens, width):
    it = iter(gens)
    active = []
    more = True
    while True:
        if more and len(active) < width:
            try:
                active.append(next(it))
            except StopIteration:
                more = False
        if not active:
            break
        for g_ in list(active):
            try:
                next(g_)
            except StopIteration:
                active.remove(g_)


def build_program(debug=None):
    debug = debug or {}
    nc = bass.Bass("TRN2", target_bir_lowering=False)

    def din(name, shape, dt=F32):
        return nc.dram_tensor(name, list(shape), dt, kind="ExternalInput").ap()

    x = din("x", [NTOK, D])
    wsrc = din("wsrc", [NUNITS, 128, 4096])
    gmix_d = din("gmix_b", [128, D])
    gmlp_d = din("gmlp_b", [128, D])
    gfin_d = din("gfin_b", [128, D])
    bu_d = din("bu_b", [128, 512])
    bcol_d = din("bcol", [128, 40])
    cw_d = din("cwcol", [128, 32])
    lamr_d = din("lamr", [128, 32])
    lami_d = din("lami", [128, 32])
    logdt_d = din("logdt", [128, 32])
    bre_d = din("bre", [128, 512])
    bim_d = din("bim", [128, 512])
    cre_d = din("cre", [128, 512])
    cim_d = din("cim", [128, 512])
    dcol_d = din("dcol", [128, 32])
    ident_d = din("ident", [128, 128])
    jswap_d = din("jswap", [128, 128])
    causal_d = din("causal", [128, 2048])
    hm_d = din("hm", [128, 8])
    evec_d = din("evec", [128, NE])
    out = nc.dram_tensor("out", [NTOK, D], F32, kind="ExternalOutput").ap()
    dbg_out = {}
    for name, shape in debug.items():
        ddt = BF16 if name in ("tq", "pd", "md", "ws") else F32
        dbg_out[name] = nc.dram_tensor("dbg_" + name, list(shape), ddt, kind="ExternalOutput").ap()

    ws = nc.dram_tensor("ws_scratch", [NUNITS, 128, 4096], BF16).ap()
    tqd = nc.dram_tensor("tq_scratch", [32, 128, 1792], BF16).ap()
    TOFF = [0, 512, 896, 1152]
    pd = nc.dram_tensor("p_scratch", [8, 128, 4, 512], BF16).ap()
    md = nc.dram_tensor("m_scratch", [8, 128, 4, 768], BF16).ap()

    P = Prog(nc)
    es = ExitStack()

    uid = [0]

    def sb(stack, name, shape, dt):
        uid[0] += 1
        return stack.enter_context(nc.sbuf_tensor("sb%d_%s" % (uid[0], name), list(shape), dt))

    def MM(o, lhsT, rhs, start, stop, reads, writes):
        P.op("pe", lambda e: e.matmul(o, lhsT, rhs, start=start, stop=stop), reads, writes)

    def TR(o, in_, ident, reads, writes):
        P.op("pe", lambda e: e.transpose(o, in_, ident), reads, writes)

    def ACT(o, in_, func, reads, writes, bias=None, scale=None, accum=None):
        kw = {}
        if bias is not None:
            kw["bias"] = bias
        if scale is not None:
            kw["scale"] = scale
        if accum is not None:
            kw["accum_out"] = accum
        P.op("act", lambda e: e.activation(o, in_, func, **kw), reads, writes)

    def TTo(eng, o, a, b, op, reads, writes):
        P.op(eng, lambda e: e.tensor_tensor(o, a, b, op), reads, writes)

    def TS(eng, o, a, s1, s2, op0, op1, reads, writes):
        if s2 is None:
            P.op(eng, lambda e: e.tensor_scalar(o, a, s1, None, op0), reads, writes)
        else:
            P.op(eng, lambda e: e.tensor_scalar(o, a, s1, s2, op0, op1), reads, writes)

    def STT(eng, o, a, s, b, op0, op1, reads, writes):
        eng = "dve"
        P.op(eng, lambda e: e.scalar_tensor_tensor(o, a, s, b, op0, op1), reads, writes)

    def CP(eng, o, a, reads, writes):
        if eng == "act":
            P.op("act", lambda e: e.copy(o, a), reads, writes)
        else:
            P.op(eng, lambda e: e.tensor_copy(o, a), reads, writes)

    def MSET(eng, o, val, writes):
        P.op(eng, lambda e: e.memset(o, val), (), writes)

    def DMA(o, in_, reads, writes, key, eng="sp"):
        P.op(eng, lambda e: e.dma_start(out=o, in_=in_), reads, writes, dma=True, key=key)

    def DBG(name, ap, tile, idx=None):
        if name in dbg_out:
            dst = dbg_out[name] if idx is None else dbg_out[name][idx]
            DMA(dst, ap, [tile], [], key="dbg")

    UNIT_NEL = [4096] + [3072] * 8 + [4096] * 26

    def make_caster(stg, stb, STG_, STB_):
        loaded = []
        casted = []

        def step(ui):
            if ui is not None:
                s = ui % 2
                nel = UNIT_NEL[ui]
                DMA(stg[s][:, 0:nel], wsrc[ui][:, 0:nel], [], [STG_[s][0]], key="stg%d" % s, eng="act")
            if casted:
                uj, sj, nj = casted.pop(0)
                DMA(ws[uj][:, 0:nj], stb[sj][:, 0:nj], [STB_[sj]], [], key="wsst%d" % sj, eng="act")
            if loaded:
                uk, sk, nk = loaded.pop(0)
                CP("act", stb[sk][:, 0:nk], stg[sk][:, 0:nk], [STG_[sk][0]], [STB_[sk]])
                casted.append((uk, sk, nk))
            if ui is not None:
                loaded.append((ui, s, nel))

        def flush():
            while loaded or casted:
                step(None)

        return step, flush

    psum = [es.enter_context(nc.psum_tensor("psb%d" % i, [128, 512], F32)) for i in range(8)]
    psT = [Tile("psum%d" % i) for i in range(8)]
    bank_ctr = [0]

    def bank():
        i = bank_ctr[0] % 8
        bank_ctr[0] += 1
        return psum[i], psT[i]

    g_mix = sb(es, "g_mix", [128, D], F32)
    g_mlp = sb(es, "g_mlp", [128, D], F32)
    g_fin = sb(es, "g_fin", [128, D], F32)
    bu_b = sb(es, "bu_b", [128, 512], F32)
    bcol = sb(es, "bcol", [128, 40], F32)
    cw = sb(es, "cw", [128, 32], F32)
    ident_f = sb(es, "ident_f", [128, 128], F32)
    ident_b = sb(es, "ident_b", [128, 128], BF16)
    cT = Tile("consts")

    with ExitStack() as ss:
        lamr = sb(ss, "lamr", [128, 32], F32)
        lami = sb(ss, "lami", [128, 32], F32)
        logdt = sb(ss, "logdt", [128, 32], F32)
        bre = sb(ss, "bre", [128, 32, 16], F32)
        bim = sb(ss, "bim", [128, 32, 16], F32)
        cre = sb(ss, "cre", [128, 32, 16], F32)
        cim = sb(ss, "cim", [128, 32, 16], F32)
        dcol = sb(ss, "dcol", [128, 32], F32)
        jswap = sb(ss, "jswap", [128, 128], F32)
        causal = sb(ss, "causal", [128, 4, 512], F32)
        hm = sb(ss, "hm", [128, 8], F32)
        evec = sb(ss, "evec", [128, NE], F32)

        loads = [(g_mix[:], gmix_d), (g_mlp[:], gmlp_d), (g_fin[:], gfin_d), (bu_b[:], bu_d),
                 (bcol[:], bcol_d), (cw[:], cw_d), (ident_f[:], ident_d), (lamr[:], lamr_d),
                 (lami[:], lami_d), (logdt[:], logdt_d),
                 (bre[:], bre_d.rearrange("p (g c) -> p g c", c=16)),
                 (bim[:], bim_d.rearrange("p (g c) -> p g c", c=16)),
                 (cre[:], cre_d.rearrange("p (g c) -> p g c", c=16)),
                 (cim[:], cim_d.rearrange("p (g c) -> p g c", c=16)),
                 (dcol[:], dcol_d), (jswap[:], jswap_d),
                 (causal[:], causal_d.rearrange("p (k c) -> p k c", c=512)),
                 (hm[:], hm_d), (evec[:], evec_d)]
        for (o_, i_) in loads:
            DMA(o_, i_, [], [cT], key="G:consts")
        CP("dve", ident_b[:], ident_f[:], [cT], [cT])

        dt_ = sb(ss, "dt_", [128, 32], F32)
        th = sb(ss, "th", [128, 32], F32)
        lrd = sb(ss, "lrd", [128, 32], F32)
        Ar = sb(ss, "Ar", [128, 32, NE], F32)
        Ai = sb(ss, "Ai", [128, 32, NE], F32)
        er = sb(ss, "er", [128, 32], F32)
        t0 = sb(ss, "t0", [128, 32], F32)
        t1 = sb(ss, "t1", [128, 32], F32)
        rden = sb(ss, "rden", [128, 32], F32)
        qre = sb(ss, "qre", [128, 32], F32)
        qim = sb(ss, "qim", [128, 32], F32)
        Bbre = sb(ss, "Bbre", [128, 32, 16], F32)
        Bbim = sb(ss, "Bbim", [128, 32, 16], F32)
        tmpb = sb(ss, "tmpb", [128, 32, 16], F32)
        colA31 = sb(ss, "colA31", [128, 32], F32)
        colBk = sb(ss, "colBk", [128, 32, 6], F32)
        stk_tabs = {n_: sb(ss, n_, [128, 32, 32], F32) for n_ in ("UaN", "UbN", "Va", "Vb", "Wa", "Wb")}
        tb = Tile("tables")
        stg = [sb(ss, "stg%d" % i, [128, 4096], F32) for i in range(2)]
        stb = [sb(ss, "stb%d" % i, [128, 4096], BF16) for i in range(2)]
        STG_ = [[Tile("stg%d_%d" % (i, j)) for j in range(5)] for i in range(2)]
        STB_ = [Tile("stb%d" % i) for i in range(2)]
        emit_cast, flush_cast_store = make_caster(stg, stb, STG_, STB_)
        s1_next = [0]

        def s1_cast():
            if s1_next[0] < 8 and STOP_AFTER != "S1":
                emit_cast(s1_next[0])
                s1_next[0] += 1

        st = ExitStack()
        ang = sb(st, "ang", [128, 32, NE], F32)
        lnm = sb(st, "lnm", [128, 32, NE], F32)
        rs = sb(st, "rs", [128, 32, NE], F32)
        ACT(dt_[:], logdt[:], AF.Exp, [cT], [tb])
        TTo("dve", th[:], lami[:], dt_[:], ALU.mult, [cT, tb], [tb])
        TTo("dve", lrd[:], lamr[:], dt_[:], ALU.mult, [cT, tb], [tb])
        ev_b = evec[:].unsqueeze(1).broadcast_to([128, 32, NE])
        TTo("dve", ang[:], th[:].unsqueeze(2).broadcast_to([128, 32, NE]), ev_b, ALU.mult, [tb, cT], [tb])
        TTo("dve", lnm[:], lrd[:].unsqueeze(2).broadcast_to([128, 32, NE]), ev_b, ALU.mult, [tb, cT], [tb])
        ACT(lnm[:], lnm[:], AF.Exp, [tb], [tb])
        s1_cast()
        ki = sb(st, "ki", [128, 32, NE], mybir.dt.int32)
        kf = sb(st, "kf", [128, 32, NE], F32)
        C1 = 6.28125
        C2 = TWO_PI - 6.28125

        def sin_reduced(dst, shift):
            if shift != 0.0:
                TS("dve", rs[:], ang[:], shift, None, ALU.add, None, [tb], [tb])
                src_ = rs
            else:
                src_ = ang
            s1_cast()
            TS("dve", kf[:], src_[:], 1.0 / TWO_PI, None, ALU.mult, None, [tb], [tb])
            CP("dve", ki[:], kf[:], [tb], [tb])
            CP("dve", kf[:], ki[:], [tb], [tb])
            STT("dve", rs[:], kf[:], -C1, src_[:], ALU.mult, ALU.add, [tb], [tb])
            STT("dve", rs[:], kf[:], -C2, rs[:], ALU.mult, ALU.add, [tb], [tb])
            s1_cast()
            ACT(dst[:], rs[:], AF.Sin, [tb], [tb], scale=SIN_SC)

        sin_reduced(Ar, 0.5 * np.pi)
        sin_reduced(Ai, 0.0)
        s1_cast()
        TTo("dve", Ar[:], Ar[:], lnm[:], ALU.mult, [tb], [tb])
        TTo("dve", Ai[:], Ai[:], lnm[:], ALU.mult, [tb], [tb])
        ei = Ai[:, :, 33]
        TS("dve", er[:], Ar[:, :, 33], -1.0, None, ALU.add, None, [tb], [tb])
        TTo("dve", t0[:], lamr[:], lamr[:], ALU.mult, [cT], [tb])
        TTo("dve", t1[:], lami[:], lami[:], ALU.mult, [cT], [tb])
        TTo("dve", t0[:], t0[:], t1[:], ALU.add, [tb], [tb])
        P.op("dve", lambda e: e.reciprocal(rden[:], t0[:]), [tb], [tb])
        TTo("dve", t0[:], er[:], lamr[:], ALU.mult, [tb, cT], [tb])
        TTo("dve", t1[:], ei, lami[:], ALU.mult, [tb, cT], [tb])
        TTo("dve", t0[:], t0[:], t1[:], ALU.add, [tb], [tb])
        TTo("dve", qre[:], t0[:], rden[:], ALU.mult, [tb], [tb])
        TTo("dve", t0[:], ei, lamr[:], ALU.mult, [tb, cT], [tb])
        TTo("dve", t1[:], er[:], lami[:], ALU.mult, [tb, cT], [tb])
        TTo("dve", t0[:], t0[:], t1[:], ALU.subtract, [tb], [tb])
        TTo("dve", qim[:], t0[:], rden[:], ALU.mult, [tb], [tb])
        s1_cast()
        qre_b = qre[:].unsqueeze(2).broadcast_to([128, 32, 16])
        qim_b = qim[:].unsqueeze(2).broadcast_to([128, 32, 16])
        TTo("dve", Bbre[:], bre[:], qre_b, ALU.mult, [cT, tb], [tb])
        TTo("dve", tmpb[:], bim[:], qim_b, ALU.mult, [cT, tb], [tb])
        TTo("dve", Bbre[:], Bbre[:], tmpb[:], ALU.subtract, [tb], [tb])
        TTo("dve", Bbim[:], bim[:], qre_b, ALU.mult, [cT, tb], [tb])
        TTo("dve", tmpb[:], bre[:], qim_b, ALU.mult, [cT, tb], [tb])
        TTo("dve", Bbim[:], Bbim[:], tmpb[:], ALU.add, [tb], [tb])
        s1_cast()
        m0, m1, msg, nm1, nm0 = (hm[:, 0:1], hm[:, 1:2], hm[:, 2:3], hm[:, 3:4], hm[:, 4:5])

        def stack_tab(name, lo, sa, src_a, sb_, src_b):
            t = stk_tabs[name]
            TS("dve", t[:], src_a[:, :, lo:lo + 32], sa, None, ALU.mult, None, [tb, cT], [tb])
            STT("dve", t[:], src_b[:, :, lo:lo + 32], sb_, t[:], ALU.mult, ALU.add, [tb, cT], [tb])
            return t

        UaN = stack_tab("UaN", 0, m0, Ar, nm1, Ai)
        UbN = stack_tab("UbN", 0, m0, Ai, m1, Ar)
        Va = stack_tab("Va", 32, m0, Ar, m1, Ai)
        Vb = stack_tab("Vb", 32, nm0, Ai, m1, Ar)
        Wa = stack_tab("Wa", 33, m0, Ar, nm1, Ai)
        Wb = stack_tab("Wb", 33, nm0, Ai, nm1, Ar)
        TS("dve", colA31[:], Ar[:, :, 63], msg, None, ALU.mult, None, [tb, cT], [tb])
        TS("dve", colBk[:], Ai[:, :, 65:71], msg, None, ALU.mult, None, [tb, cT], [tb])

        P.flush()
        st.close()
        stop_s1 = STOP_AFTER == "S1"
        NB = 3
        ABt = [sb(ss, "ABt%d" % i, [128, 32, 16], F32) for i in range(NB)]
        rhsf = [sb(ss, "rhsf%d" % i, [128, 640], F32) for i in range(NB)]
        tqa = [sb(ss, "tqa%d" % i, [128, 32, 16], F32) for i in range(NB)]
        tqc = [sb(ss, "tqc%d" % i, [128, 32, 16], F32) for i in range(NB)]
        tq2 = [sb(ss, "tq2%d" % i, [128, 32, 16], F32) for i in range(NB)]
        tdg = [sb(ss, "tdg%d" % i, [128, 1, 128], F32) for i in range(NB)]
        tmk = [sb(ss, "tmk%d" % i, [128, 6, 128], F32) for i in range(NB)]
        tm2 = [sb(ss, "tm2%d" % i, [128, 6, 128], F32) for i in range(NB)]
        TM2_ = [Tile("tm2%d" % i) for i in range(NB)]
        opsb = [sb(ss, "opsb%d" % i, [128, OPSW], BF16) for i in range(NB)]
        ABT_ = [Tile("ABt%d" % i) for i in range(NB)]
        RHS_ = [Tile("rhsf%d" % i) for i in range(NB)]
        TQA_ = [Tile("tqa%d" % i) for i in range(NB)]
        TQC_ = [Tile("tqc%d" % i) for i in range(NB)]
        TQ2_ = [Tile("tq2%d" % i) for i in range(NB)]
        TDG_ = [Tile("tdg%d" % i) for i in range(NB)]
        TMK_ = [Tile("tmk%d" % i) for i in range(NB)]
        OPSB_ = [Tile("opsb%d" % i) for i in range(NB)]
        def bc_c(t, g):
            return t[:, g, :].unsqueeze(1).broadcast_to([128, 32, 16])

        def bc_i(t, g):
            return t[:, g, :].unsqueeze(2).broadcast_to([128, 32, 16])

        cdiag = causal[:, 0, 0:128]
        print("SBUF free in S2 scope:", nc.sbuf_bytes_remaining)

        def gen_group(g):
            s = g % NB
            ab, rf, qa, qc, q2, q3, td, tm, ob = ABt[s], rhsf[s], tqa[s], tqc[s], tq2[s], tqc[s], tdg[s], tmk[s], opsb[s]
            TTo("dve", ab[:], bc_c(Bbre, g), bc_i(UaN, g), ALU.mult, [tb], [ABT_[s]])
            TTo("dve", qa[:], bc_c(Bbim, g), bc_i(UbN, g), ALU.mult, [tb], [TQA_[s]])
            TTo("dve", ab[:], ab[:], qa[:], ALU.subtract, [TQA_[s], ABT_[s]], [ABT_[s]])
            ca = rf[:, 0:512].rearrange("p (j c) -> p j c", c=16)
            TTo("pool", ca, bc_c(cre, g), bc_i(Va, g), ALU.mult, [tb, cT], [RHS_[s]])
            TTo("pool", qc[:], bc_c(cim, g), bc_i(Vb, g), ALU.mult, [tb, cT], [TQC_[s]])
            TTo("pool", ca, ca, qc[:], ALU.add, [TQC_[s], RHS_[s]], [RHS_[s]])
            ACT(rf[:, 512:640], ident_f[:], AF.Copy, [cT, tb], [RHS_[s]], scale=colA31[:, g:g + 1])
            STT("dve", rf[:, 512:640], jswap[:], Ai[:, g, 63:64], rf[:, 512:640], ALU.mult, ALU.add,
                [cT, tb, RHS_[s]], [RHS_[s]])
            ACT(td[:, 0, :], ident_f[:], AF.Copy, [cT], [TDG_[s]], scale=dcol[:, g:g + 1])
            TTo("pool", q2[:], bc_c(cre, g), bc_i(Wa, g), ALU.mult, [tb, cT], [TQ2_[s]])
            TTo("pool", q3[:], bc_c(cim, g), bc_i(Wb, g), ALU.mult, [tb, cT], [TQC_[s]])
            TTo("dve", ob[:, 1280:1792].rearrange("p (j c) -> p j c", c=16), q2[:], q3[:], ALU.add,
                [TQ2_[s], TQC_[s]], [OPSB_[s]])
            id6 = ident_f[:].unsqueeze(1).broadcast_to([128, 6, 128])
            js6 = jswap[:].unsqueeze(1).broadcast_to([128, 6, 128])
            TTo("pool", tm[:], id6, Ar[:, g, 65:71].unsqueeze(2).broadcast_to([128, 6, 128]), ALU.mult,
                [cT, tb], [TMK_[s]])
            TTo("dve", tm2[s][:], js6,
                colBk[:, g, :].unsqueeze(2).broadcast_to([128, 6, 128]), ALU.mult, [cT, tb], [TM2_[s]])
            TTo("dve", ob[:, 3072:3840].rearrange("p (l s) -> p l s", s=128), tm[:], tm2[s][:], ALU.add,
                [TMK_[s], TM2_[s]], [OPSB_[s]])
            yield
            abf = ab[:].rearrange("p i c -> p (i c)")
            for k in range(4):
                pb1, pT1 = bank()
                MM(pb1[:, 128 * k:512], abf[:, 128 * k:128 * k + 128], rf[:, 128 * k:512], True, False,
                   [ABT_[s], RHS_[s]], [pT1])
                MM(pb1[:, 128 * k:128 * k + 128], ident_f[:], td[:, 0, :], False, True, [cT, TDG_[s]], [pT1])
                if k < 3:
                    CP("act", ob[:, TOFF[k] + 128:TOFF[k] + 512 - 128 * k], pb1[:, 128 * (k + 1):512], [],
                       [pT1, OPSB_[s]])
                TTo("dve", ob[:, TOFF[k]:TOFF[k] + 128], pb1[:, 128 * k:128 * k + 128], cdiag,
                    ALU.mult, [cT], [pT1, OPSB_[s]])
                pb2, pT2 = bank()
                MM(pb2[:, 0:128], abf[:, 128 * k:128 * k + 128], rf[:, 512:640], True, True,
                   [ABT_[s], RHS_[s]], [pT2])
                CP("act", ob[:, 2560 + 128 * k:2688 + 128 * k], pb2[:, 0:128], [], [pT2, OPSB_[s]])
            yield
            DMA(tqd[g], ob[:, 0:1792], [OPSB_[s]], [], key="opsst%d_0" % s)
            DMA(pd[g // 4][:, g % 4, :], ob[:, 2560:3072], [OPSB_[s]], [], key="opsst%d_1" % s)
            DMA(md[g // 4][:, g % 4, :], ob[:, 3072:3840], [OPSB_[s]], [], key="opsst%d_2" % s)
            if g % 2 == 0 and s1_next[0] < 24:
                emit_cast(s1_next[0])
                s1_next[0] += 1

        interleave((gen_group(g) for g in range(0 if stop_s1 else 32)), NB)
        while s1_next[0] < 24 and not stop_s1:
            emit_cast(s1_next[0])
            s1_next[0] += 1
        flush_cast_store()
        P.flush()
        dT = Tile("dbgT")
        if "tq" in dbg_out:
            DMA(dbg_out["tq"], tqd, [], [dT], key="dbg")
        if "pd" in dbg_out:
            DMA(dbg_out["pd"], pd, [], [dT], key="dbg")
        if "md" in dbg_out:
            DMA(dbg_out["md"], md, [], [dT], key="dbg")
        if "ws" in dbg_out:
            for u_ in range(NUNITS):
                DMA(dbg_out["ws"][u_], ws[u_], [], [dT], key="dbg")
        P.flush()

    stop = STOP_AFTER
    npass = 0 if stop in ("S1", "S2") else (1 if stop is not None else NPASS)
    import os as _os
    pass_list = [int(v) for v in _os.environ['KERNEL_PASSES'].split(',')] if 'KERNEL_PASSES' in _os.environ else list(range(npass))
    for pas in pass_list:
        with ExitStack() as ps_:
            zT = sb(ps_, "zT", [128, 4, PTOK], BF16)
            ZT_ = [Tile("zT%d" % i) for i in range(8)]
            with ExitStack() as ab_:
                UZ = sb(ab_, "UZ", [128, 32 * 512], BF16)
                UZ_ = Tile("UZ")
                Ut = UZ[:].rearrange("p (g i c) -> p g i c", g=32, i=32)
                with ExitStack() as a_:
                    Wu = sb(a_, "Wu", [128, 8, 512], BF16)
                    WU_ = Tile("Wu")
                    xa = [sb(a_, "xa%d" % i, [128, D], F32) for i in range(4)]
                    XA_ = [Tile("xa%d" % i) for i in range(4)]
                    xna = [sb(a_, "xna%d" % i, [128, D], BF16) for i in range(4)]
                    XNA_ = [Tile("xna%d" % i) for i in range(4)]
                    sqj = sb(a_, "sqj", [128, D], BF16)
                    SQJ_ = Tile("sqj")
                    ssq = [sb(a_, "ssq%d" % i, [128, 2], F32) for i in range(4)]
                    SSQ_ = [Tile("ssq%d" % i) for i in range(4)]
                    xnTa = [sb(a_, "xnTa%d" % i, [128, 8, 128], BF16) for i in range(4)]
                    XNTA_ = [Tile("xnTa%d" % i) for i in range(4)]

                    DMA(Wu[:], ws[0].rearrange("p (k c) -> p k c", c=512), [], [WU_], key="wu")
                    xp = x[pas * PTOK:(pas + 1) * PTOK, :].rearrange("(s i) d -> s i d", i=32)
                    if pas == 0:
                        stgA = [sb(a_, "stgA%d" % i, [128, 4096], F32) for i in range(2)]
                        stbA = [sb(a_, "stbA%d" % i, [128, 4096], BF16) for i in range(2)]
                        castA_next = [24]
                        castA, flushA = make_caster(stgA, stbA, [[Tile("stgA%d" % i)] for i in range(2)],
                                                    [Tile("stbA%d" % i) for i in range(2)])

                    def gen_tile(i):
                        s = i % 4
                        if pas == 0 and i % 3 == 1 and castA_next[0] < NUNITS:
                            castA(castA_next[0])
                            castA_next[0] += 1
                        DMA(xa[s][:], xp[:, i, :], [], [XA_[s]], key="xa%d" % s)
                        ACT(sqj[:], xa[s][:], AF.Square, [XA_[s]], [SQJ_, SSQ_[s]], accum=ssq[s][:, 0:1])
                        ACT(ssq[s][:, 1:2], ssq[s][:, 0:1], AF.Sqrt, [SSQ_[s]], [SSQ_[s]], bias=1e-6, scale=1.0 / D)
                        P.op("dve", lambda e, o_=ssq[s][:, 1:2]: e.reciprocal(o_, o_), [SSQ_[s]], [SSQ_[s]])
                        STT("dve", xna[s][:], xa[s][:], ssq[s][:, 1:2], g_mix[:], ALU.mult, ALU.mult,
                            [XA_[s], SSQ_[s], cT], [XNA_[s]])
                        yield
                        pb, pT = bank()
                        pbv = pb[:].bitcast(BF16).rearrange("p (k s) -> p k s", k=8)
                        for k in range(8):
                            TR(pbv[:, k, :], xna[s][:, 128 * k:128 * k + 128], ident_b[:], [XNA_[s], cT], [pT])
                        CP("act", xnTa[s][:], pbv, [], [pT, XNTA_[s]])
                        yield
                        pb2, pT2 = bank()
                        for k in range(8):
                            MM(pb2[:], xnTa[s][:, k, :], Wu[:, k, :], k == 0, k == 7, [XNTA_[s], WU_], [pT2])
                        TTo("dve", Ut[:, :, i, :], pb2[:].rearrange("p (g c) -> p g c", c=16),
                            bu_b[:].rearrange("p (g c) -> p g c", c=16), ALU.add, [cT], [pT2, UZ_])

                    interleave((gen_tile(i) for i in range(32)), 3)
                    if pas == 0:
                        while castA_next[0] < NUNITS:
                            castA(castA_next[0])
                            castA_next[0] += 1
                        flushA()
                    P.flush()

                if stop == "A":
                    continue
                with ExitStack() as b_:
                    U = sb(b_, "U", [128, 32, 4, 128], BF16)
                    U_ = [Tile("U%d" % g) for g in range(32)]
                    S32 = sb(b_, "S32", [128, 32, 128], F32)
                    Sbf = sb(b_, "Sbf", [128, 32, 128], BF16)
                    Sprev = sb(b_, "Sprev", [128, 32, 128], BF16)
                    S32_ = [Tile("S32_%d" % i) for i in range(8)]
                    SB_ = [Tile("Sb_%d" % i) for i in range(8)]
                    SPV_ = [Tile("Sprev%d" % i) for i in range(8)]
                    pl = [sb(b_, "pl%d" % i, [128, 4, 512], BF16) for i in range(2)]
                    PL_ = [Tile("pl%d" % i) for i in range(2)]
                    ml = [sb(b_, "ml%d" % i, [128, 4, 768], BF16) for i in range(3)]
                    ML_ = [Tile("ml%d" % i) for i in range(3)]
                    tql = [sb(b_, "tql%d" % i, [128, 1792], BF16) for i in range(4)]
                    TQL_ = [Tile("tql%d" % i) for i in range(4)]

                    for g in range(32):
                        if g % 2 == 0:
                            pb, pT = bank()
                            pbv = pb[:].bitcast(BF16).rearrange("p (k s) -> p k s", k=8)
                        for k in range(4):
                            TR(pbv[:, (g % 2) * 4 + k, :],
                               Ut[:, g, 8 * k:8 * k + 8, :].rearrange("p i c -> p (i c)"), ident_b[:], [UZ_, cT], [pT])
                        if g % 2 == 1:
                            CP("act" if (g // 2) % 2 == 0 else "dve",
                               U[:, g - 1:g + 1, :, :].rearrange("p g k s -> p (g k) s"), pbv, [],
                               [pT, U_[g - 1], U_[g]])
                    for gb in range(8):
                        s = gb % 2
                        DMA(pl[s][:], pd[gb], [], [PL_[s]], key="pl%d" % s)
                        pb, pT = bank()
                        for gi in range(4):
                            g = gb * 4 + gi
                            for k in range(4):
                                MM(pb[:, 128 * gi:128 * gi + 128], pl[s][:, gi, 128 * k:128 * k + 128], U[:, g, k, :],
                                   k == 0, k == 3, [PL_[s], U_[g]], [pT])
                        CP("act", S32[:, 4 * gb:4 * gb + 4, :].rearrange("p g s -> p (g s)"), pb[:], [],
                           [pT, S32_[gb]])
                        CP("dve", Sbf[:, 4 * gb:4 * gb + 4, :].rearrange("p g s -> p (g s)"), pb[:], [], [pT, SB_[gb]])
                    def gen_b3(gb):
                        s = gb % 3
                        DMA(ml[s][:], md[gb], [], [ML_[s]], key="ml%d" % s)
                        yield
                        for lev in range(6):
                            sh = 1 << lev
                            pb, pT = bank()
                            for gi in range(4):
                                g = gb * 4 + gi
                                for q in range(2):
                                    c0 = 128 * gi + 64 * q
                                    MM(pb[:, c0 + sh:c0 + 64], ml[s][:, gi, 128 * lev:128 * lev + 128],
                                       Sbf[:, g, 64 * q:64 * q + 64 - sh], True, True, [ML_[s], SB_[gb]], [pT])
                            sv = S32[:, 4 * gb:4 * gb + 4, :].rearrange("p g (q s) -> p (g q) s", q=2)[:, :, sh:64]
                            pv = pb[:].rearrange("p (gq s) -> p gq s", s=64)[:, :, sh:64]
                            TTo("dve", sv, sv, pv, ALU.add, [S32_[gb]], [pT, S32_[gb]])
                            if lev < 5:
                                CP("act", Sbf[:, 4 * gb:4 * gb + 4, :],
                                   S32[:, 4 * gb:4 * gb + 4, :], [S32_[gb]], [SB_[gb]])
                            else:
                                spv = Sprev[:, 4 * gb:4 * gb + 4, :].rearrange("p g (q s) -> p (g q) s", q=2)
                                s3v = S32[:, 4 * gb:4 * gb + 4, :].rearrange("p g (q s) -> p (g q) s", q=2)
                                MSET("dve", spv[:, :, 0:1], 0.0, [SPV_[gb]])
                                CP("act", spv[:, :, 1:64], s3v[:, :, 0:63], [S32_[gb]], [SPV_[gb]])
                            yield

                    interleave((gen_b3(gb) for gb in range(8)), 3)
                    Zt = UZ[:].rearrange("p (j g c) -> p j g c", j=32, g=32)
                    yt = [sb(b_, "yt%d" % i, [128, 512], F32) for i in range(2)]
                    y2 = [sb(b_, "y2%d" % i, [128, 512], F32) for i in range(2)]
                    YT_ = [Tile("yt%d" % i) for i in range(2)]
                    Y2_ = [Tile("y2%d" % i) for i in range(2)]
                    GC = float(2.0 * np.sqrt(2.0 / np.pi))
                    zTv = zT[:].rearrange("p c (s j) -> p c j s", j=32)
                    Ztf = UZ[:].rearrange("p (j ch) -> p j ch", j=32)
                    cp_eng = ["act", "dve", "pool"]
                    cp_ctr = [0]

                    def b5(cc):
                        for jb in range(4):
                            pb, pT = bank()
                            pbv = pb[:].bitcast(BF16).rearrange("p (k s) -> p k s", k=8)
                            for jj in range(8):
                                j = jb * 8 + jj
                                TR(pbv[:, jj, :], Ztf[:, j, 128 * cc:128 * cc + 128], ident_b[:], [UZ_, cT], [pT])
                            ce = cp_eng[cp_ctr[0] % 2]
                            cp_ctr[0] += 1
                            CP(ce, zTv[:, cc, 8 * jb:8 * jb + 8, :], pbv, [], [pT] + ZT_)

                    def gen_b4(g):
                        s = g % 4
                        DMA(tql[s][:], tqd[g], [], [TQL_[s]], key="tql%d" % s)
                        yield
                        yield
                        pb, pT = bank()
                        for k in range(4):
                            MM(pb[:, 128 * k:512], U[:, g, k, :], tql[s][:, TOFF[k]:TOFF[k] + 512 - 128 * k], k == 0, False,
                               [TQL_[s], U_[g]], [pT])
                        MM(pb[:], Sprev[:, g, :], tql[s][:, 1280:1792], False, True, [TQL_[s], SPV_[g // 4]], [pT])
                        ACT(Zt[:, :, g, :], pb[:].rearrange("p (j c) -> p j c", c=16), AF.Gelu_apprx_tanh, [],
                            [pT, UZ_])
                        if g >= 10 and (g - 10) % 8 == 0:
                            b5((g - 10) // 8)

                    interleave((gen_b4(g) for g in range(32)), 3)
                    b5(3)
                    if "zT" in dbg_out and pas == 0:
                        zf = sb(b_, "zf_dbg", [128, 4, 512], F32)
                        ZF_ = Tile("zf")
                        for tt in range(8):
                            CP("dve", zf[:], zT[:, :, 512 * tt:512 * tt + 512], ZT_, [ZF_])
                            DMA(dbg_out["zT"][:, :, 512 * tt:512 * tt + 512], zf[:], [ZF_], [], key="dbg")
                    P.flush()

            if stop == "B":
                continue
            with ExitStack() as cs:
                xt = sb(cs, "xt", [128, 4, D], F32)
                XT_ = Tile("xt")
                h = sb(cs, "h", [128, 4, D], F32)
                H_ = [Tile("h%d" % a) for a in range(4)]
                sqj = sb(cs, "sqjc", [128, D], BF16)
                SQJ_ = Tile("sqjc")
                ssq = sb(cs, "ssqc", [128, 24], F32)
                SSQ_ = [[Tile("ssqc%d_%d" % (st_, a)) for a in range(4)] for st_ in range(3)]
                hnT = sb(cs, "hnT", [128, 8, TT], BF16)
                HNT_ = [Tile("hnT%d" % a) for a in range(4)]
                xn = [sb(cs, "xn%d" % i, [128, D], BF16) for i in range(2)]
                XN_ = [Tile("xn%d" % i) for i in range(2)]
                xnT = sb(cs, "xnT", [128, 8, TT], BF16)
                XNT_ = [Tile("xnT%d" % a) for a in range(4)]
                cg = [sb(cs, "cg%d" % i, [128, TT], F32) for i in range(2)]
                CG_ = [Tile("cg%d" % i) for i in range(2)]
                cv = [sb(cs, "cv%d" % i, [128, TT + 2], F32) for i in range(2)]
                CV_ = [Tile("cv%d" % i) for i in range(2)]
                carry = sb(cs, "carry", [128, 8, 2], F32)
                CAR_ = [Tile("carry%d" % m) for m in range(8)]
                acc = [sb(cs, "acc%d" % i, [128, TT], F32) for i in range(2)]
                ACC_ = [Tile("acc%d" % i) for i in range(2)]
                G = sb(cs, "G", [128, 8, TT], BF16)
                G_ = [Tile("G%d" % m) for m in range(8)]
                sgb = [sb(cs, "sgb%d" % i, [128, TT], BF16) for i in range(2)]
                sgs = [sb(cs, "sgs%d" % i, [128, TT], BF16) for i in range(2)]
                sgc = [sb(cs, "sgc%d" % i, [128, TT], BF16) for i in range(2)]
                SGB_ = [Tile("sgb%d" % i) for i in range(2)]
                SGS_ = [Tile("sgs%d" % i) for i in range(2)]
                SGC_ = [Tile("sgc%d" % i) for i in range(2)]
                ta = [sb(cs, "ta%d" % i, [128, TT], F32) for i in range(2)]
                tbb = [sb(cs, "tbb%d" % i, [128, TT], F32) for i in range(2)]
                TA_ = [Tile("ta%d" % i) for i in range(2)]
                TB_ = [Tile("tbb%d" % i) for i in range(2)]
                mg = sb(cs, "mg", [128, 8, TT], BF16)
                MG_ = [Tile("mg%d" % m) for m in range(8)]
                rl = [sb(cs, "rl%d" % i, [128, TT], F32) for i in range(2)]
                RL_ = [Tile("rl%d" % i) for i in range(2)]
                h2 = sb(cs, "h2", [128, 16, TT], BF16)
                H2_ = [Tile("h2_%d" % f) for f in range(16)]
                ring = [sb(cs, "ring%d" % i, [128, 4096], BF16) for i in range(RING)]
                RING_ = [Tile("ring%d" % i) for i in range(RING)]

                seq_units = []
                for tt in range(8):
                    seq_units += [1 + m for m in range(8)] + [9 + m for m in range(8)] + [17, 18]
                    for r in range(2):
                        seq_units += [19 + 4 * r + u for u in range(4)]
                        seq_units += [27 + 4 * r + 2 * hh + q for hh in range(2) for q in range(2)]
                nload = [0]
                nuse = [0]

                def prefetch(upto):
                    while nload[0] < min(upto, len(seq_units)):
                        i_ = nload[0]
                        s_ = i_ % RING
                        nel_ = 3072 if 1 <= seq_units[i_] <= 8 else 4096
                        DMA(ring[s_][:, 0:nel_], ws[seq_units[i_]][:, 0:nel_], [], [RING_[s_]], key="ring%d" % s_)
                        nload[0] += 1

                def next_unit(expect):
                    i_ = nuse[0]
                    assert seq_units[i_] == expect, (seq_units[i_], expect)
                    prefetch(i_ + RING - 1)
                    nuse[0] += 1
                    s_ = i_ % RING
                    return ring[s_], RING_[s_]

                def rms_norm(src, SRC_, gain, dst, DST_, st_, a):
                    col = 8 * st_ + a
                    SQ = SSQ_[st_][a]
                    ACT(sqj[:], src, AF.Square, [SRC_], [SQJ_, SQ], accum=ssq[:, col:col + 1])
                    ACT(ssq[:, col + 4:col + 5], ssq[:, col:col + 1], AF.Sqrt, [SQ], [SQ], bias=1e-6, scale=1.0 / D)
                    P.op("dve", lambda e, o_=ssq[:, col + 4:col + 5]: e.reciprocal(o_, o_), [SQ], [SQ])
                    STT("dve", dst, src, ssq[:, col + 4:col + 5], gain[:], ALU.mult, ALU.mult,
                        [SRC_, SQ, cT], [DST_])

                def to_feature_major(a, src, SRC_, dstT, DST_):
                    pb, pT = bank()
                    pbv = pb[:].bitcast(BF16).rearrange("p (k s) -> p k s", k=8)
                    for k in range(8):
                        TR(pbv[:, k, :], src[:, 128 * k:128 * k + 128], ident_b[:], [SRC_, cT], [pT])
                    CP("act" if a % 2 == 0 else "dve", dstT[:, :, 128 * a:128 * a + 128], pbv, [], [pT, DST_])

                def load_x(tt):
                    t0_ = pas * PTOK + tt * TT
                    DMA(xt[:], x[t0_:t0_ + TT, :].rearrange("(a p) d -> p a d", p=128), [], [XT_], key="xt")

                xnx = [sb(cs, "xnx%d" % i, [128, D], BF16) for i in range(4)]
                XNX_ = [Tile("xnx%d" % i) for i in range(4)]

                def c2_norm_only():
                    for a in range(4):
                        rms_norm(xt[:, a, :], XT_, g_mix, xnx[a][:], XNX_[a], 0, a)

                def c2_transposes():
                    for a in range(4):
                        to_feature_major(a, xnx[a], XNX_[a], xnT, XNT_[a])

                def c2_norm_x():
                    c2_norm_only()
                    c2_transposes()

                print("SBUF free in C scope:", nc.sbuf_bytes_remaining)
                load_x(0)
                c2_norm_x()
                ntile_c = stop[1] if isinstance(stop, tuple) else 8
                for tt in range(ntile_c):
                    t0_ = pas * PTOK + tt * TT
                    zsl = slice(TT * tt, TT * tt + TT)
                    for m in range(8):
                        s = m % 2
                        wu, WU = next_unit(1 + m)
                        wv = wu[:, 0:3072].rearrange("p (k c) -> p k c", c=384)
                        pbs = []
                        for r in range(3):
                            pb, pT = bank()
                            for k in range(8):
                                MM(pb[:], wv[:, k, 128 * r:128 * r + 128], xnT[:, k, :], k == 0, k == 7,
                                   [WU] + XNT_, [pT])
                            pbs.append((pb, pT))
                        (pbB, pTB), (pbC, pTC), (pbV, pTV) = pbs
                        ACT(cg[s][:], pbC[:], AF.Identity, [cT], [pTC, CG_[s]], bias=bcol[:, 8 + m:9 + m])
                        if tt % 4 == 0:
                            MSET("pool", cv[s][:, 0:2], 0.0, [CV_[s]])
                        else:
                            CP("pool", cv[s][:, 0:2], carry[:, m, :], [CAR_[m]], [CV_[s]])
                        STT("dve", cv[s][:, 2:TT + 2], pbV[:], bcol[:, 16 + m:17 + m], cg[s][:], ALU.add, ALU.mult,
                            [cT, CG_[s]], [pTV, CV_[s]])
                        CP("pool", carry[:, m, :], cv[s][:, TT:TT + 2], [CV_[s]], [CAR_[m]])
                        TS("pool", acc[s][:], cv[s][:, 2:TT + 2], cw[:, 4 * m + 2:4 * m + 3], cw[:, 4 * m + 3:4 * m + 4],
                           ALU.mult, ALU.add, [CV_[s], cT], [ACC_[s]])
                        STT("pool", acc[s][:], cv[s][:, 1:TT + 1], cw[:, 4 * m + 1:4 * m + 2], acc[s][:], ALU.mult, ALU.add,
                            [CV_[s], cT, ACC_[s]], [ACC_[s]])
                        STT("dve", acc[s][:], cv[s][:, 0:TT], cw[:, 4 * m:4 * m + 1], acc[s][:], ALU.mult, ALU.add,
                            [CV_[s], cT, ACC_[s]], [ACC_[s]])
                        STT("dve", G[:, m, :], pbB[:], bcol[:, m:m + 1], acc[s][:], ALU.add, ALU.mult,
                            [cT, ACC_[s]], [pTB, G_[m]])
                    for m in range(8):
                        s = m % 2
                        wu, WU = next_unit(9 + m)
                        wco = wu[:, 0:1024].rearrange("p (k c) -> p k c", c=128)
                        wa = wu[:, 1024:1536].rearrange("p (k c) -> p k c", c=128)
                        wb = wu[:, 1536:2048].rearrange("p (k c) -> p k c", c=128)
                        wgs = wu[:, 2048:3072].rearrange("p (k c) -> p k c", c=128)
                        wgc = wu[:, 3072:4096].rearrange("p (k c) -> p k c", c=128)
                        pbb, pTb = bank()
                        for k in range(4):
                            MM(pbb[:], wb[:, k, :], zT[:, k, zsl], k == 0, k == 3, [WU, ZT_[tt]], [pTb])
                        pbgs, pTgs = bank()
                        for k in range(8):
                            MM(pbgs[:], wgs[:, k, :], xnT[:, k, :], k == 0, k == 7, [WU] + XNT_, [pTgs])
                        pbgc, pTgc = bank()
                        for k in range(8):
                            MM(pbgc[:], wgc[:, k, :], xnT[:, k, :], k == 0, k == 7, [WU] + XNT_, [pTgc])
                        pba, pTa = bank()
                        for k in range(4):
                            MM(pba[:], wa[:, k, :], zT[:, k, zsl], k == 0, k == 3, [WU, ZT_[tt]], [pTa])
                        pby, pTy = bank()
                        for k in range(8):
                            MM(pby[:], wco[:, k, :], G[:, k, :], k == 0, k == 7, [WU] + G_, [pTy])
                        ACT(sgb[s][:], pbb[:], AF.Sigmoid, [], [pTb, SGB_[s]])
                        ACT(sgs[s][:], pbgs[:], AF.Sigmoid, [cT], [pTgs, SGS_[s]], bias=bcol[:, 24 + m:25 + m])
                        ACT(sgc[s][:], pbgc[:], AF.Sigmoid, [cT], [pTgc, SGC_[s]], bias=bcol[:, 32 + m:33 + m])
                        TTo("dve", ta[s][:], pba[:], sgb[s][:], ALU.mult, [SGB_[s]], [pTa, TA_[s]])
                        TTo("pool", ta[s][:], ta[s][:], sgs[s][:], ALU.mult, [TA_[s], SGS_[s]], [TA_[s]])
                        TTo("dve", tbb[s][:], pby[:], sgc[s][:], ALU.mult, [SGC_[s]], [pTy, TB_[s]])
                        TTo("pool", mg[:, m, :], ta[s][:], tbb[s][:], ALU.add, [TA_[s], TB_[s]], [MG_[m]])
                    wo = [next_unit(17), next_unit(18)]
                    for a in range(4):
                        for hh in range(2):
                            wu, WU = wo[hh]
                            wv = wu[:].rearrange("p (k c) -> p k c", c=512)
                            pb, pT = bank()
                            for k in range(8):
                                MM(pb[:], mg[:, k, 128 * a:128 * a + 128], wv[:, k, :], k == 0, k == 7,
                                   [WU, MG_[k]], [pT])
                            TTo("dve", h[:, a, 512 * hh:512 * hh + 512], pb[:], xt[:, a, 512 * hh:512 * hh + 512],
                                ALU.add, [XT_], [pT, H_[a]])
                        rms_norm(h[:, a, :], H_[a], g_mlp, xn[a % 2][:], XN_[a % 2], 1, a)
                        if a >= 1:
                            to_feature_major(a - 1, xn[(a - 1) % 2], XN_[(a - 1) % 2], hnT, HNT_[a - 1])
                    to_feature_major(3, xn[1], XN_[1], hnT, HNT_[3])
                    if tt + 1 < ntile_c:
                        load_x(tt + 1)
                        c2_norm_only()
                    for r in range(2):
                        for u in range(4):
                            wu, WU = next_unit(19 + 4 * r + u)
                            wv = wu[:].rearrange("p (k c) -> p k c", c=512)
                            for fi in range(4):
                                f = 4 * u + fi
                                s = f % 2
                                pb, pT = bank()
                                for k in range(8):
                                    MM(pb[:], wv[:, k, 128 * fi:128 * fi + 128], hnT[:, k, :], k == 0, k == 7,
                                       [WU] + HNT_, [pT])
                                ACT(rl[s][:], pb[:], AF.Relu, [], [pT, RL_[s]])
                                TTo("pool" if f % 4 == 3 else "dve", h2[:, f, :], rl[s][:], rl[s][:], ALU.mult,
                                    [RL_[s]], [H2_[f]])
                        if r == 0 and tt + 1 < ntile_c:
                            c2_transposes()
                        for hh in range(2):
                            bks = [bank() for _ in range(4)]
                            for q in range(2):
                                wu, WU = next_unit(27 + 4 * r + 2 * hh + q)
                                wv = wu[:].rearrange("p (k c) -> p k c", c=512)
                                for fi in range(8):
                                    f = 8 * q + fi
                                    for a in range(4):
                                        MM(bks[a][0][:], h2[:, f, 128 * a:128 * a + 128], wv[:, fi, :], f == 0, f == 15,
                                           [WU, H2_[f]], [bks[a][1]])
                            for a in range(4):
                                hs = h[:, a, 512 * hh:512 * hh + 512]
                                TTo("dve", hs, bks[a][0][:], hs, ALU.add, [H_[a]], [bks[a][1], H_[a]])
                    for a in range(4):
                        rms_norm(h[:, a, :], H_[a], g_fin, h[:, a, :], H_[a], 2, a)
                    DMA(out[t0_:t0_ + TT, :].rearrange("(a p) d -> p a d", p=128), h[:], H_, [], key="out")
                P.flush()
    es.close()
    return nc


def _consts():
    ident = np.eye(128, dtype=np.float32)
    jswap = np.zeros((128, 128), np.float32)
    for p in range(128):
        jswap[p, (p + 64) % 128] = 1.0
    causal = np.zeros((128, 4, 512), np.float32)
    for k in range(4):
        for p in range(128):
            i = 8 * k + p // 16
            causal[p, k, 16 * i:] = 1.0
    hm = np.zeros((128, 8), np.float32)
    hm[:64, 0] = 1.0
    hm[64:, 1] = 1.0
    hm[:, 2] = hm[:, 0] - hm[:, 1]
    hm[:, 3] = -hm[:, 1]
    hm[:, 4] = -hm[:, 0]
    ev = np.concatenate([-np.arange(32), np.arange(33), 32 * 2 ** np.arange(6)]).astype(np.float32)
    evec = np.tile(ev[None, :], (128, 1))
    return ident, jswap, causal.reshape(128, 2048), hm, evec


def _pack_weights(inp):
    f = lambda a: np.asarray(a, dtype=np.float32)
    w_in, w_a, w_b = f(inp["w_in"])[0], f(inp["w_glu_a"])[0], f(inp["w_glu_b"])[0]
    w_co, w_o, w_1, w_2 = f(inp["w_conv_out"])[0], f(inp["w_out"])[0], f(inp["w_ff1"])[0], f(inp["w_ff2"])[0]

    def kview(w, r0, nr, c0, ncol):
        return w[r0:r0 + nr, c0:c0 + ncol].reshape(nr // 128, 128, ncol).transpose(1, 0, 2)

    ws = np.zeros((NUNITS, 128, 4096), np.float32)
    ws[0] = kview(w_in, 0, 1024, 0, 512).reshape(128, 4096)
    for m in range(8):
        ws[1 + m, :, :3072] = np.concatenate(
            [kview(w_in, 0, 1024, 512 + 1024 * r + 128 * m, 128) for r in range(3)], axis=2).reshape(128, 3072)
        ws[9 + m] = np.concatenate([
            kview(w_co, 0, 1024, 128 * m, 128).reshape(128, 1024),
            kview(w_a, 0, 512, 128 * m, 128).reshape(128, 512),
            kview(w_b, 0, 512, 128 * m, 128).reshape(128, 512),
            kview(w_in, 0, 1024, 3584 + 128 * m, 128).reshape(128, 1024),
            kview(w_in, 0, 1024, 4608 + 128 * m, 128).reshape(128, 1024)], axis=1)
    for h_ in range(2):
        ws[17 + h_] = kview(w_o, 0, 1024, 512 * h_, 512).reshape(128, 4096)
    for u in range(8):
        ws[19 + u] = kview(w_1, 0, 1024, 512 * u, 512).reshape(128, 4096)
    for r in range(2):
        for h_ in range(2):
            for q in range(2):
                ws[27 + 4 * r + 2 * h_ + q] = kview(w_2, 2048 * r + 1024 * q, 1024, 512 * h_, 512).reshape(128, 4096)
    return ws


def _prep_shared(inp):
    f = lambda a: np.ascontiguousarray(np.asarray(a, dtype=np.float32))
    ident, jswap, causal, hm, evec = _consts()
    b_in = f(inp["b_in"])[0]
    conv_w = f(inp["conv_w"])[0]
    conv_b = f(inp["conv_b"])[0]
    cwcol = np.zeros((128, 8, 4), np.float32)
    for k in range(3):
        cwcol[:, :, k] = conv_w[k].reshape(8, 128).T
    cwcol[:, :, 3] = conv_b.reshape(8, 128).T
    dup = lambda a: np.concatenate([a, a], axis=0)
    sh = {
        "wsrc": _pack_weights(inp),
        "gmix_b": f(np.broadcast_to(f(inp["norm_mix_g"])[0][None, :], (128, D))),
        "gmlp_b": f(np.broadcast_to(f(inp["norm_mlp_g"])[0][None, :], (128, D))),
        "gfin_b": f(np.broadcast_to(f(inp["norm_final_g"])[None, :], (128, D))),
        "bu_b": f(np.broadcast_to(b_in[None, 0:512], (128, 512))),
        "bcol": f(b_in[512:].reshape(40, 128).T),
        "cwcol": f(cwcol.reshape(128, 32)),
        "lamr": f(dup(f(inp["lam_re"])[0].T)), "lami": f(dup(f(inp["lam_im"])[0].T)),
        "logdt": f(np.broadcast_to(f(inp["log_dt"])[0][None, :], (128, 32))),
        "bre": f(dup(np.transpose(f(inp["ssm_b_re"])[0], (1, 0, 2))).reshape(128, 512)),
        "bim": f(dup(np.transpose(f(inp["ssm_b_im"])[0], (1, 0, 2))).reshape(128, 512)),
        "cre": f(dup(np.transpose(f(inp["ssm_c_re"])[0], (2, 0, 1))).reshape(128, 512)),
        "cim": f(dup(np.transpose(f(inp["ssm_c_im"])[0], (2, 0, 1))).reshape(128, 512)),
        "dcol": f(np.tile(f(inp["ssm_d"])[0].reshape(32, 16).T, (8, 1))),
        "ident": ident, "jswap": jswap, "causal": causal, "hm": hm, "evec": evec,
    }
    return sh


_NC_CACHE = {}


def kernel(**inputs):
    x = np.asarray(inputs["x"], dtype=np.float32)
    shared = _prep_shared(inputs)
    key = tuple(sorted(DEBUG.items()))
    nc = build_program(DEBUG)
    in_maps = []
    for c in range(NCORES):
        m = dict(shared)
        m["x"] = np.ascontiguousarray(x[4 * c:4 * c + 4].reshape(NTOK, D))
        in_maps.append(m)
    res = run_bass_kernel_spmd(nc, in_maps, core_ids=list(range(NCORES)))
    kernel.last_results = res
    outs = [np.asarray(r["out"]).reshape(4, 2048, D) for r in res.results]
    return np.concatenate(outs, axis=0).astype(np.float32)
```
